# Optimizing a Trainium2 kernel written in Bass

```python
import jax, jax.numpy as jnp
from jax import lax
import numpy as np

D_MODEL = 2048
BATCH = 2
SEQ = 4096
DEPTH = 2
DEC_BATCH = 16
DEC_SEQ = 64
PAST_LEN = 4096

CHUNK = 64
N_HEADS = 16
HEAD_DIM = D_MODEL // N_HEADS
Q_BLOCK = 128
MLP_CHUNK = 128
N_GROUPS = 8
GROUP_DIM = D_MODEL // N_GROUPS
D_FF = 11 * D_MODEL // 4
CONV_W = 3
PLE_DIM = 256
N_SB_LAYERS = (DEPTH + 1) // 2
N_MLP_LAYERS = DEPTH // 2
EPS = 1e-6

kernel_name = "stickbreak_chunkmlp_convffn_stream_step"


def rmsnorm(x, g):
    xf = x.astype(jnp.float32)
    y = xf * lax.rsqrt(jnp.mean(xf * xf, axis=-1, keepdims=True) + EPS)
    return y.astype(x.dtype) * g


def stick_breaking(q, k, v, q_pos, k_pos):
    z = jnp.einsum("bqhd,bkhd->bhqk", q, k).astype(jnp.float32) * (HEAD_DIM ** -0.5)
    mask = k_pos[None, :] < q_pos[:, None]
    log_keep = jnp.where(mask, jax.nn.log_sigmoid(-z), 0.0)
    log_after = lax.cumsum(log_keep, axis=3, reverse=True) - log_keep
    w = jnp.where(mask, jnp.exp(jax.nn.log_sigmoid(z) + log_after), 0.0)
    return jnp.einsum("bhqk,bkhd->bqhd", w.astype(v.dtype), v)


def sb_mixer(hn, w_qkv, w_o, cache_k, cache_v):
    B, T, _ = hn.shape
    qkv = (hn @ w_qkv).reshape(B, T, 3, N_HEADS, HEAD_DIM)
    q, k, v = qkv[:, :, 0], qkv[:, :, 1], qkv[:, :, 2]
    if cache_k is None:
        blocks = []
        for lo in range(0, T, Q_BLOCK):
            hi = lo + Q_BLOCK
            blocks.append(stick_breaking(q[:, lo:hi], k[:, :hi], v[:, :hi],
                                         jnp.arange(lo, hi), jnp.arange(hi)))
        o = jnp.concatenate(blocks, axis=1)
    else:
        past = cache_k.shape[1]
        kk = jnp.concatenate([cache_k, k], axis=1)
        vv = jnp.concatenate([cache_v, v], axis=1)
        o = stick_breaking(q, kk, vv, past + jnp.arange(T), jnp.arange(past + T))
    return o.reshape(B, T, D_MODEL) @ w_o, k, v


def chunk_mlp(hn, w_in, g_v, w_s, b_s, w_o):
    B, T, _ = hn.shape
    u, v = jnp.split(jax.nn.gelu(hn @ w_in), 2, axis=-1)
    v = rmsnorm(v, g_v)
    L = min(T, MLP_CHUNK)
    blk = jnp.arange(MLP_CHUNK) // CHUNK
    w_sp = jnp.where(blk[None, :] <= blk[:, None], w_s, 0.0)[:, :L, :L]
    vc = v.reshape(B, T // L, L, N_GROUPS, GROUP_DIM)
    mix = jnp.einsum("gts,bcsgd->bctgd", w_sp, vc) + b_s[:, :L].T[:, :, None]
    y = u * mix.reshape(B, T, D_MODEL)
    return y @ w_o, v


def conv_ffn(hn, conv_state, w_up, conv_w, conv_b, w_down):
    a, g = jnp.split(hn @ w_up, 2, axis=-1)
    B, T, _ = a.shape
    if conv_state is None:
        conv_state = jnp.zeros((B, CONV_W - 1, D_FF), a.dtype)
    ext = jnp.concatenate([conv_state, a], axis=1)
    ac = conv_b + ext[:, 0:T] * conv_w[0]
    for j in range(1, CONV_W):
        ac = ac + ext[:, j:j + T] * conv_w[j]
    y = (jax.nn.gelu(ac) * g) @ w_down
    return y, ext[:, T:]


def trunk(x, p, cache_k, cache_v, state_conv, g_mix, w_qkv, w_o_sb, w_in_mlp, g_v_mlp,
          w_s_mlp, b_s_mlp, w_o_mlp, g_ffn, w_up, conv_w, conv_b, w_down, g_ple, w_ple,
          w_ple_gate, g_final):
    h = x
    ks, vs, mlp_vs, convs = [], [], [], []
    for i in range(DEPTH):
        hn = rmsnorm(h, g_mix[i])
        j = i // 2
        if i % 2 == 0:
            y, k, v = sb_mixer(hn, w_qkv[j], w_o_sb[j],
                               None if cache_k is None else cache_k[j],
                               None if cache_v is None else cache_v[j])
            ks.append(k)
            vs.append(v)
        else:
            y, vrow = chunk_mlp(hn, w_in_mlp[j], g_v_mlp[j], w_s_mlp[j], b_s_mlp[j], w_o_mlp[j])
            mlp_vs.append(vrow)
        h = h + y
        y, cs = conv_ffn(rmsnorm(h, g_ffn[i]), None if state_conv is None else state_conv[i],
                         w_up[i], conv_w[i], conv_b[i], w_down[i])
        convs.append(cs)
        h = h + y
        h = h + jax.nn.sigmoid(rmsnorm(h, g_ple[i]) @ w_ple_gate[i]) * (p[i] @ w_ple[i])
    return rmsnorm(h, g_final), jnp.stack(ks), jnp.stack(vs), mlp_vs, jnp.stack(convs)


def setup_inputs(seed: int = 0) -> dict:
    key = jax.random.key(seed)
    ks = iter(jax.random.split(key, 32))
    f32 = jnp.float32

    def nrm(shape, scale=1.0):
        return jax.random.normal(next(ks), shape, f32) * scale

    def gain(shape):
        return 1.0 + 0.02 * jax.random.normal(next(ks), shape, f32)

    return {
        "x_prompt": nrm((BATCH, SEQ, D_MODEL)),
        "x_sample": nrm((DEC_BATCH, DEC_SEQ, D_MODEL)),
        "p_prompt": nrm((DEPTH, BATCH, SEQ, PLE_DIM)),
        "p_sample": nrm((DEPTH, DEC_BATCH, DEC_SEQ, PLE_DIM)),
        "cache_k": nrm((N_SB_LAYERS, DEC_BATCH, PAST_LEN, N_HEADS, HEAD_DIM)),
        "cache_v": nrm((N_SB_LAYERS, DEC_BATCH, PAST_LEN, N_HEADS, HEAD_DIM)),
        "state_conv": nrm((DEPTH, DEC_BATCH, CONV_W - 1, D_FF)),
        "g_mix": gain((DEPTH, D_MODEL)),
        "w_qkv": nrm((N_SB_LAYERS, D_MODEL, 3 * D_MODEL), D_MODEL ** -0.5),
        "w_o_sb": nrm((N_SB_LAYERS, D_MODEL, D_MODEL), D_MODEL ** -0.5),
        "w_in_mlp": nrm((N_MLP_LAYERS, D_MODEL, 2 * D_MODEL), D_MODEL ** -0.5),
        "g_v_mlp": gain((N_MLP_LAYERS, D_MODEL)),
        "w_s_mlp": nrm((N_MLP_LAYERS, N_GROUPS, MLP_CHUNK, MLP_CHUNK), MLP_CHUNK ** -0.5),
        "b_s_mlp": gain((N_MLP_LAYERS, N_GROUPS, MLP_CHUNK)),
        "w_o_mlp": nrm((N_MLP_LAYERS, D_MODEL, D_MODEL), D_MODEL ** -0.5),
        "g_ffn": gain((DEPTH, D_MODEL)),
        "w_up": nrm((DEPTH, D_MODEL, 2 * D_FF), D_MODEL ** -0.5),
        "conv_w": nrm((DEPTH, CONV_W, D_FF), CONV_W ** -0.5),
        "conv_b": nrm((DEPTH, D_FF), 0.02),
        "w_down": nrm((DEPTH, D_FF, D_MODEL), D_FF ** -0.5),
        "g_ple": gain((DEPTH, D_MODEL)),
        "w_ple": nrm((DEPTH, PLE_DIM, D_MODEL), PLE_DIM ** -0.5),
        "w_ple_gate": nrm((DEPTH, D_MODEL, D_MODEL), D_MODEL ** -0.5),
        "g_final": gain((D_MODEL,)),
    }


def reference(x_prompt, x_sample, p_prompt, p_sample, cache_k, cache_v, state_conv, g_mix,
              w_qkv, w_o_sb, w_in_mlp, g_v_mlp, w_s_mlp, b_s_mlp, w_o_mlp, g_ffn, w_up,
              conv_w, conv_b, w_down, g_ple, w_ple, w_ple_gate, g_final):
    y_prompt, k_prompt, v_prompt, _, conv_prompt = trunk(
        x_prompt, p_prompt, None, None, None, g_mix, w_qkv, w_o_sb, w_in_mlp, g_v_mlp,
        w_s_mlp, b_s_mlp, w_o_mlp, g_ffn, w_up, conv_w, conv_b, w_down, g_ple, w_ple,
        w_ple_gate, g_final)
    y_sample, k_sample, v_sample, mlp_v_sample, conv_sample = trunk(
        x_sample, p_sample, cache_k, cache_v, state_conv, g_mix, w_qkv, w_o_sb, w_in_mlp,
        g_v_mlp, w_s_mlp, b_s_mlp, w_o_mlp, g_ffn, w_up, conv_w, conv_b, w_down, g_ple,
        w_ple, w_ple_gate, g_final)
    mlpv_sample = jnp.stack(mlp_v_sample)
    return (y_prompt, y_sample, k_prompt, v_prompt, k_sample, v_sample, mlpv_sample,
            conv_prompt, conv_sample)
```

```python
import numpy as np
import concourse.bass as bass
import concourse.mybir as mybir
from concourse.bass_utils import run_bass_kernel_spmd
from contextlib import ExitStack

F32 = mybir.dt.float32
BF16 = mybir.dt.bfloat16
AF = mybir.ActivationFunctionType
ALU = mybir.AluOpType
COMPUTE = ("pe", "act", "dve", "pool")

D = 2048
KC = 16
H = 16
DFF = 5632
FC = 44
PL = 256
SCALE = float(128 ** -0.5)
EPS = 1e-6
BIG = 1.0e9
NKB = 32
TA = 768
TB = 640
GC = 1.5957691216057308


class Slot:
    __slots__ = ("sem", "cnt", "idx")

    def __init__(self, idx):
        self.sem = None
        self.cnt = 0
        self.idx = idx


class Res:
    __slots__ = ("name", "last_w", "readers", "slot")

    def __init__(self, name):
        self.name = name
        self.last_w = None
        self.readers = []
        self.slot = None


class Op:
    __slots__ = ("eng", "fn", "deps", "is_dma", "sem_res", "sem_val", "seq", "need_inc")

    def __init__(self, eng, fn, is_dma):
        self.eng = eng
        self.fn = fn
        self.deps = []
        self.is_dma = is_dma
        self.sem_res = None
        self.sem_val = 0
        self.seq = 0
        self.need_inc = False


class Prog:
    def __init__(self, nc):
        self.nc = nc
        self.ops = []
        self.stack = ExitStack()
        self.bar = []
        self.last = {}
        self.dmas_since = []
        self.slots = []
        self.free_slots = []
        self.active = []

    def sbuf(self, name, shape, dtype):
        return self.stack.enter_context(self.nc.sbuf_tensor(name, list(shape), dtype))

    def psum(self, name, shape, dtype):
        return self.stack.enter_context(self.nc.psum_tensor(name, list(shape), dtype))

    def barrier(self):
        self.bar = list(self.last.values()) + self.dmas_since
        self.dmas_since = []
        for r in self.active:
            self.free_slots.append(r.slot)
            r.slot = None
        self.active = []

    def op(self, eng, fn, reads=(), writes=(), dma=False, sem_on=None):
        o = Op(eng, fn, dma)
        deps = list(self.bar)
        for r in reads:
            if r.last_w is not None:
                deps.append(r.last_w)
        for r in writes:
            if r.last_w is not None:
                deps.append(r.last_w)
            deps.extend(r.readers)
        seen = set()
        for d in deps:
            if id(d) not in seen:
                seen.add(id(d))
                o.deps.append(d)
        for r in reads:
            if not dma:
                r.readers = [x for x in r.readers if x.is_dma or x.eng != eng]
            r.readers.append(o)
        for r in writes:
            r.last_w = o
            r.readers = []
        if dma:
            if sem_on.slot is None:
                if self.free_slots:
                    sem_on.slot = self.free_slots.pop()
                else:
                    sem_on.slot = Slot(len(self.slots))
                    self.slots.append(sem_on.slot)
                self.active.append(sem_on)
            sem_on.slot.cnt += 16
            o.sem_res = sem_on.slot
            o.sem_val = sem_on.slot.cnt
            self.dmas_since.append(o)
        else:
            self.last[eng] = o
        self.ops.append(o)
        return o

    def emit(self):
        nc = self.nc
        for o in self.ops:
            for d in o.deps:
                if not d.is_dma:
                    if d.eng == "pe" and o.eng == "pe" and not o.is_dma:
                        continue
                    d.need_inc = True
        seqc = {e: 0 for e in COMPUTE}
        for o in self.ops:
            if not o.is_dma and o.need_inc:
                seqc[o.eng] += 1
                o.seq = seqc[o.eng]
        esem = {e: self.stack.enter_context(nc.semaphore("s_" + e)) for e in COMPUTE}
        dma_res = self.slots
        for sl in self.slots:
            sl.sem = self.stack.enter_context(nc.semaphore("d_%d" % sl.idx))
        self.n_sems = len(dma_res) + 4
        per = {"pe": [], "act": [], "dve": [], "pool": [], "sp": []}
        for o in self.ops:
            per[o.eng].append(o)

        def run(engname, engobj):
            waited = {}
            for o in per[engname]:
                for d in o.deps:
                    if d.is_dma:
                        key, val, sem = ("d", id(d.sem_res)), d.sem_val, d.sem_res.sem
                    else:
                        if d.eng == "pe" and engname == "pe" and not o.is_dma:
                            continue
                        key, val, sem = ("e", d.eng), d.seq, esem[d.eng]
                    if waited.get(key, 0) >= val:
                        continue
                    waited[key] = val
                    engobj.wait_ge(sem, val)
                ins = o.fn(engobj)
                if o.is_dma:
                    ins.then_inc(o.sem_res.sem, 16)
                elif o.need_inc:
                    ins.then_inc(esem[o.eng], 1)
            if engname == "sp":
                for r in dma_res:
                    if waited.get(("d", id(r)), 0) < r.cnt:
                        engobj.wait_ge(r.sem, r.cnt)
                for e in COMPUTE:
                    if seqc[e] > 0:
                        engobj.wait_ge(esem[e], seqc[e])

        with nc.Block() as block:
            @block.tensor
            def _(e):
                run("pe", e)

            @block.scalar
            def _(e):
                run("act", e)

            @block.vector
            def _(e):
                run("dve", e)

            @block.gpsimd
            def _(e):
                run("pool", e)

            @block.sync
            def _(e):
                run("sp", e)
        self.stack.close()


class _Stop(Exception):
    pass


def build_nc(stop_after=None):
    nc = bass.Bass("TRN2", target_bir_lowering=False)
    P = Prog(nc)

    def chk(tag):
        if stop_after == tag:
            raise _Stop()

    def din(name, shape, dt=F32):
        return nc.dram_tensor(name, list(shape), dt, kind="ExternalInput").ap()

    def dout(name, shape, dt=F32):
        return nc.dram_tensor(name, list(shape), dt, kind="ExternalOutput").ap()

    xk = din("xk", [4096, D]); kpos_d = din("kpos", [128, 34])
    xm = din("xm", [1280, D]); xs_d = din("xs", [128, D])
    pm = din("pm", [2, 1280, PL]); psm = din("psm", [2, 128, PL])
    qpos_d = din("qpos", [128, 1280 + 512])
    kposs_d = din("kposs", [128, 34])
    ck_d = din("ck", [2, 4096, D]); cv_d = din("cv", [2, 4096, D])
    sconv_d = din("sconv", [2, 2, 128, FC, 2])
    w_qkv = din("w_qkv", [D, 3 * D]); w_osb = din("w_osb", [D, D])
    w_in = din("w_in", [D, 2 * D]); w_omlp = din("w_omlp", [D, D])
    w_up = din("w_up", [2, D, 2 * DFF]); w_down = din("w_down", [2, DFF, D])
    w_ple = din("w_ple", [2, PL, D]); w_gate = din("w_gate", [2, D, D])
    w_sT = din("w_sT", [8, 128, 128])
    gvec_d = din("gvec", [128, 8, KC]); gmix0b_d = din("gmix0b", [128, D]); gvb_d = din("gvb", [128, D])
    convw_d = din("convw", [2, 128, FC, 3]); convb_d = din("convb", [2, 128, FC])
    bs_d = din("bs", [1, 8, 128])

    y_m = dout("y_m", [1024, D]); y_s = dout("y_s", [128, D])
    k_m = dout("k_m", [1024, D]); v_m = dout("v_m", [1024, D])
    k_s = dout("k_s", [128, D]); v_s = dout("v_s", [128, D])
    mlpv_s = dout("mlpv_s", [128, D])
    conv_o = dout("conv_o", [2, 6, DFF])
    KT = nc.dram_tensor("KTs", [H, 128, 4096], BF16, kind="ExternalOutput").ap()
    VS = nc.dram_tensor("VSs", [4096, D], BF16, kind="ExternalOutput").ap()
    r_KT = Res("KT"); r_VS = Res("VS")

    hT_t = P.sbuf("hT", [128, KC * TA], F32)
    hnT_t = P.sbuf("hnT", [128, KC * TA], BF16)
    yb_t = P.sbuf("yb", [128, 24 * TA], BF16)
    wb_t = [P.sbuf(f"wb{i}", [128, 8192], BF16) for i in range(2)]
    r_wb = [Res(f"wb{i}") for i in range(2)]
    scr_t = P.sbuf("scr", [128, 10240], F32)
    cst_t = P.sbuf("cst", [128, 2048], F32)
    qpos_t = P.sbuf("qpos_sb", [128, 1280 + 512], F32)
    wsp_t = P.sbuf("wsp", [128, 2 * 8 * 128], BF16)
    r_cst = Res("cst")
    ps = [P.psum(f"ps{i}", [128, 512], F32) for i in range(8)]
    r_ps = [Res(f"ps{i}") for i in range(8)]

    def v3(ap, c):
        return ap.rearrange("p (c n) -> p c n", c=c)

    identf = cst_t[:, 0:128]
    identb = cst_t[:, 128:192].bitcast(BF16)
    trib = cst_t[:, 192:256].bitcast(BF16)
    onesb = cst_t[:, 256:320].bitcast(BF16)
    gvec = v3(cst_t[:, 320:448], 8)
    kpos = cst_t[:, 448:482]
    kposs = cst_t[:, 482:516]
    convw = [v3(cst_t[:, 516 + l * 132: 516 + (l + 1) * 132], FC) for l in range(2)]
    convb = [cst_t[:, 780 + l * FC: 780 + (l + 1) * FC] for l in range(2)]
    convsave = [v3(cst_t[:, 868 + l * 88: 868 + (l + 1) * 88], FC) for l in range(2)]
    aout = [v3(cst_t[:, 1044 + l * 264: 1044 + (l + 1) * 264], FC) for l in range(2)]
    sconv = [[v3(cst_t[:, 1572 + (l * 2 + b) * 88: 1572 + (l * 2 + b + 1) * 88], FC) for b in range(2)] for l in range(2)]
    tmpc = cst_t[:, 1924:2048]
    bs_hl = P.sbuf("bs_hl", [2, 2 * 8 * 128], BF16)
    bs_f = scr_t[0:2, 4096:6144]
    ones2 = P.sbuf("ones2", [2, 128], BF16)
    r_bs = Res("bs")

    sp_, act_, dve_, pool_, pe_ = "sp", "act", "dve", "pool", "pe"

    def setup_consts():
        P.op(pool_, lambda e: e.memset(cst_t[:, 0:320], 0.0), writes=[r_cst])
        P.op(pool_, lambda e: e.affine_select(out=identf, in_=identf, pattern=[[-1, 128]], compare_op=ALU.not_equal,
                                              fill=1.0, base=0, channel_multiplier=1), reads=[r_cst], writes=[r_cst])
        P.op(dve_, lambda e: e.tensor_copy(out=identb, in_=identf), reads=[r_cst], writes=[r_cst])
        P.op(pool_, lambda e: e.memset(tmpc[:, 0:128 - 4], 1.0), reads=[r_cst], writes=[r_cst])
        P.op(pool_, lambda e: e.memset(scr_t[:, 0:128], 1.0), writes=[r_cst], reads=[r_cst])
        P.op(pool_, lambda e: e.affine_select(out=scr_t[:, 0:128], in_=scr_t[:, 0:128], pattern=[[-1, 128]],
                                              compare_op=ALU.is_gt, fill=0.0, base=0, channel_multiplier=1),
             reads=[r_cst], writes=[r_cst])
        P.op(dve_, lambda e: e.tensor_copy(out=trib, in_=scr_t[:, 0:128]), reads=[r_cst], writes=[r_cst])
        P.op(pool_, lambda e: e.memset(scr_t[:, 128:256], 1.0), writes=[r_cst], reads=[r_cst])
        P.op(dve_, lambda e: e.tensor_copy(out=onesb, in_=scr_t[:, 128:256]), reads=[r_cst], writes=[r_cst])
        P.op(dve_, lambda e: e.tensor_copy(out=ones2[:], in_=scr_t[0:2, 128:256]), reads=[r_cst], writes=[r_cst])
        loads = [(gvec, gvec_d), (kpos, kpos_d), (kposs, kposs_d), (qpos_t[:], qpos_d)]
        for l in range(2):
            loads += [(convw[l], convw_d[l]), (convb[l], convb_d[l])]
            for b in range(2):
                loads.append((sconv[l][b], sconv_d[l, b]))
        for (dst, src) in loads:
            P.op(sp_, lambda e, dst=dst, src=src: e.dma_start(out=dst, in_=src), writes=[r_cst], reads=[r_cst], dma=True, sem_on=r_cst)
        wsp = wsp_t[:].rearrange("p (v g t) -> p v g t", v=2, g=8)
        P.op(pool_, lambda e: e.memset(wsp_t[:], 0.0), writes=[r_bs])
        P.op(pool_, lambda e: e.dma_start(out=wsp[:, 0], in_=w_sT.rearrange("g s t -> s g t")), reads=[r_bs], writes=[r_bs], dma=True, sem_on=r_bs)
        P.op(pool_, lambda e: e.dma_start(out=wsp[0:64, 1, :, 0:64], in_=w_sT.rearrange("g s t -> s g t")[0:64, :, 0:64]), reads=[r_bs], writes=[r_bs], dma=True, sem_on=r_bs)
        P.op(pool_, lambda e: e.dma_start(out=wsp[64:128, 1, :, 64:128], in_=w_sT.rearrange("g s t -> s g t")[0:64, :, 0:64]), reads=[r_bs], writes=[r_bs], dma=True, sem_on=r_bs)
        P.op(pool_, lambda e: e.memset(wsp[64:128, 0, :, 0:64], 0.0), reads=[r_bs], writes=[r_bs])
        bsf4 = bs_f.rearrange("p (v g t) -> p v g t", v=2, g=8)
        P.op(sp_, lambda e: e.dma_start(out=bsf4[0:1, 0], in_=bs_d), reads=[r_bs], writes=[r_bs], dma=True, sem_on=r_bs)
        P.op(sp_, lambda e: e.dma_start(out=bsf4[0:1, 1, :, 0:64], in_=bs_d[:, :, 0:64]), reads=[r_bs], writes=[r_bs], dma=True, sem_on=r_bs)
        P.op(sp_, lambda e: e.dma_start(out=bsf4[0:1, 1, :, 64:128], in_=bs_d[:, :, 0:64]), reads=[r_bs], writes=[r_bs], dma=True, sem_on=r_bs)
        P.op(dve_, lambda e: e.tensor_copy(out=bs_hl[0:1, :], in_=bs_f[0:1, :]), reads=[r_bs], writes=[r_bs])
        P.op(dve_, lambda e: e.tensor_tensor(out=bs_f[0:1, :], in0=bs_f[0:1, :], in1=bs_hl[0:1, :], op=ALU.subtract), reads=[r_bs], writes=[r_bs])
        P.op(sp_, lambda e: e.dma_start(out=bs_f[1:2, :], in_=bs_f[0:1, :]), reads=[r_bs], writes=[r_bs], dma=True, sem_on=r_bs)
        P.op(pool_, lambda e: e.dma_start(out=bs_hl[1:2, :], in_=bs_f[1:2, :]), reads=[r_bs], writes=[r_bs], dma=True, sem_on=r_bs)
        P.barrier()

    rr = {"ev": 0, "wb": 0}

    def evac_eng():
        rr["ev"] ^= 1
        return act_ if rr["ev"] else dve_

    def copy_op(eng, out, in_, reads, writes):
        if eng == act_:
            P.op(act_, lambda e: e.copy(out=out, in_=in_), reads=reads, writes=writes)
        else:
            P.op(eng, lambda e: e.tensor_copy(out=out, in_=in_), reads=reads, writes=writes)

    def load_w(src2d, kchunks, ncols):
        i = rr["wb"]; rr["wb"] ^= 1
        view = wb_t[i][:, 0:kchunks * ncols].rearrange("p (k n) -> p k n", k=kchunks)
        srcv = src2d.rearrange("(k p) n -> p k n", p=128)
        step = 4
        for k0 in range(0, kchunks, step):
            k1 = min(kchunks, k0 + step)
            P.op(pool_, lambda e, k0=k0, k1=k1: e.dma_start(out=view[:, k0:k1, :], in_=srcv[:, k0:k1, :]),
                 writes=[r_wb[i]], dma=True, sem_on=r_wb[i])
        return view, r_wb[i]

    psrr = {"i": 0}

    def next_ps(lo=0, hi=8):
        i = psrr["i"]
        psrr["i"] = (i + 1)
        return lo + (i % (hi - lo))

    def linear_fm(wsrc, kchunks, nout, rhs3, r_rhs, tiles, evac, wcols=512):
        ntile = nout * 128 // wcols
        pend = load_w(wsrc(0, wcols), kchunks, wcols)
        for t in range(ntile):
            wv, rw = pend
            if t + 1 < ntile:
                pend = load_w(wsrc((t + 1) * wcols, wcols), kchunks, wcols)
            for mi in range(wcols // 128):
                m = t * (wcols // 128) + mi
                for (c0, n) in tiles:
                    b = next_ps()
                    for k in range(kchunks):
                        P.op(pe_, lambda e, b=b, n=n, wv=wv, k=k, mi=mi, c0=c0: e.matmul(
                            ps[b][:, 0:n], lhsT=wv[:, k, mi * 128:(mi + 1) * 128], rhs=rhs3[:, k, c0:c0 + n],
                            start=(k == 0), stop=(k == kchunks - 1)),
                            reads=[rw] + r_rhs(k), writes=[r_ps[b]])
                    evac(m, c0, n, ps[b], r_ps[b])

    def phase_kv():
        gmix0b = scr_t[:, 0:2048]
        r_g = Res("gmix0b")
        P.op(sp_, lambda e: e.dma_start(out=gmix0b, in_=gmix0b_d), writes=[r_g], dma=True, sem_on=r_g)
        xst = [scr_t[:, 2048:4096], scr_t[:, 4096:6144]]
        r_xst = [Res("xstk0"), Res("xstk1")]
        hnb = scr_t[:, 6144:7168].bitcast(BF16)
        r_hnb = Res("hnb")
        stat = scr_t[:, 7168:7176]; r_stat = Res("stat")
        junk = scr_t[:, 8200:9224].bitcast(BF16); r_junk = Res("junk")
        stg = [scr_t[:, 7432:7944], scr_t[:, 7944:8200].bitcast(BF16)]
        r_stg = [Res("stgf"), Res("stgb")]
        ktst = scr_t[:, 7176:7432].bitcast(BF16)
        r_ktst = Res("ktst")
        hk = hT_t[:].bitcast(BF16)[:, 0:KC * 1536].rearrange("p (c n) -> p c n", c=KC)
        r_hk = [Res(f"hk{b}") for b in range(12)]
        lvl = int(stop_after[2:]) if (stop_after or "").startswith("kv") and len(stop_after) > 2 else 9
        for half in range(3 if lvl == 9 else 1):
            nbl = 12 if half < 2 else 8
            if lvl < 9:
                nbl = 2
            for bl in range(nbl):
                blk = half * 12 + bl
                xi = blk % 2
                P.op(sp_, lambda e, xi=xi, blk=blk: e.dma_start(out=xst[xi], in_=xk[blk * 128:(blk + 1) * 128, :]),
                     writes=[r_xst[xi]], dma=True, sem_on=r_xst[xi])
                P.op(pool_, lambda e: e.memset(stat[:, 0:1], 0.0), writes=[r_stat])
                P.op(act_, lambda e, xi=xi: e.activation(out=junk, in_=xst[xi], func=AF.Square, accum_out=stat[:, 0:1]),
                     reads=[r_xst[xi], r_stat], writes=[r_junk, r_stat])
                P.op(act_, lambda e: e.activation(out=stat[:, 1:2], in_=stat[:, 0:1], func=AF.Sqrt, bias=EPS, scale=1.0 / D),
                     reads=[r_stat], writes=[r_stat])
                P.op(dve_, lambda e: e.reciprocal(out=stat[:, 2:3], in_=stat[:, 1:2]), reads=[r_stat], writes=[r_stat])
                P.op(dve_, lambda e, xi=xi: e.scalar_tensor_tensor(out=hnb, in0=xst[xi], scalar=stat[:, 2:3], in1=gmix0b,
                                                                   op0=ALU.mult, op1=ALU.mult),
                     reads=[r_xst[xi], r_stat, r_g], writes=[r_hnb])
                for hh in range(2 if lvl >= 2 else 0):
                    b = next_ps()
                    pb = ps[b][:].bitcast(BF16)
                    for c8 in range(8):
                        c = hh * 8 + c8
                        P.op(pe_, lambda e, pb=pb, c8=c8, c=c: e.transpose(out=pb[:, c8 * 128:(c8 + 1) * 128],
                                                                          in_=hnb[:, c * 128:(c + 1) * 128], identity=identb),
                             reads=[r_hnb, r_cst], writes=[r_ps[b]])
                    copy_op(evac_eng(), hk[:, hh * 8:(hh + 1) * 8, bl * 128:(bl + 1) * 128],
                            pb.rearrange("p (c n) -> p c n", c=8), [r_ps[b]], [r_hk[bl]])
            for ct in (range(8) if lvl == 9 else ([0, 4] if lvl >= 3 else [])):
                wv, rw = load_w(w_qkv[:, 2048 + ct * 512: 2048 + (ct + 1) * 512], KC, 512)
                for bl in range(nbl):
                    blk = half * 12 + bl
                    b = next_ps()
                    for k in range(KC):
                        P.op(pe_, lambda e, b=b, k=k, bl=bl, wv=wv: e.matmul(ps[b][:, :], lhsT=hk[:, k, bl * 128:(bl + 1) * 128],
                                                                            rhs=wv[:, k, :], start=(k == 0), stop=(k == KC - 1)),
                             reads=[rw, r_hk[bl]], writes=[r_ps[b]])
                    if blk < 8:
                        dst = (k_m if ct < 4 else v_m)[blk * 128:(blk + 1) * 128, (ct % 4) * 512:(ct % 4 + 1) * 512]
                        copy_op(act_, stg[0], ps[b][:, :], [r_ps[b]], [r_stg[0], r_ps[b]])
                        P.op(sp_, lambda e, dst=dst: e.dma_start(out=dst, in_=stg[0]), reads=[r_stg[0]], dma=True, sem_on=r_stg[0])
                    copy_op(dve_, stg[1], ps[b][:, :], [r_ps[b]], [r_stg[1], r_ps[b]])
                    if ct >= 4:
                        if lvl < 4:
                            continue
                        dst = VS[blk * 128:(blk + 1) * 128, (ct - 4) * 512:(ct - 3) * 512]
                        P.op(sp_, lambda e, dst=dst: e.dma_start(out=dst, in_=stg[1]), reads=[r_stg[1]], writes=[r_VS], dma=True, sem_on=r_stg[1])
                    else:
                        if lvl < 5:
                            continue
                        b2 = next_ps()
                        pb = ps[b2][:].bitcast(BF16)
                        for h4 in range(4):
                            P.op(pe_, lambda e, pb=pb, h4=h4: e.transpose(out=pb[:, h4 * 128:(h4 + 1) * 128],
                                                                         in_=stg[1][:, h4 * 128:(h4 + 1) * 128], identity=identb),
                                 reads=[r_stg[1], r_cst], writes=[r_ps[b2]])
                        copy_op(act_, ktst, pb[:, 0:512], [r_ps[b2]], [r_ktst])
                        dst = KT[ct * 4:(ct + 1) * 4, :, blk * 128:(blk + 1) * 128].rearrange("h d t -> d h t")
                        P.op(sp_, lambda e, dst=dst: e.dma_start(out=dst, in_=ktst.rearrange("p (h t) -> p h t", h=4)),
                             reads=[r_ktst], writes=[r_KT], dma=True, sem_on=r_ktst)
        P.barrier()

    class G:
        pass

    def make_group(gi):
        g = G()
        g.gi = gi
        g.T = TA if gi == 0 else TB
        g.hT = v3(hT_t[:, 0:KC * g.T], KC)
        g.hnT = v3(hnT_t[:, 0:KC * g.T], KC)
        g.yb = v3(yb_t[:, 0:24 * g.T], 24)
        g.r_h = [Res(f"h{gi}_{c}") for c in range(KC)]
        g.r_hn = [Res(f"hn{gi}_{c}") for c in range(KC)]
        g.r_y = [Res(f"y{gi}_{c}") for c in range(24)]
        g.tiles = [(0, 512), (512, g.T - 512)]
        g.nblk = g.T // 128
        if gi == 0:
            g.segs = [(0, 768, "zero")]
        else:
            g.segs = [(0, 512, "save"), (512, 64, "s0"), (576, 64, "s1")]
        return g

    def load_x(g):
        xst = [scr_t[:, 0:2048], scr_t[:, 2048:4096]]
        r_x = [Res("xs0"), Res("xs1")]
        for blk in range(g.nblk):
            if g.gi == 0:
                src = xm[blk * 128:(blk + 1) * 128, :]
            elif blk < 4:
                src = xm[768 + blk * 128: 768 + (blk + 1) * 128, :]
            else:
                src = xs_d
            xi = blk % 2
            P.op(sp_, lambda e, xi=xi, src=src: e.dma_start(out=xst[xi], in_=src), writes=[r_x[xi]], dma=True, sem_on=r_x[xi])
            for q in range(4):
                b = next_ps()
                for c4 in range(4):
                    c = q * 4 + c4
                    P.op(pe_, lambda e, b=b, c4=c4, c=c, xi=xi: e.transpose(out=ps[b][:, c4 * 128:(c4 + 1) * 128],
                                                                           in_=xst[xi][:, c * 128:(c + 1) * 128], identity=identf),
                         reads=[r_x[xi], r_cst], writes=[r_ps[b]])
                copy_op(evac_eng(), g.hT[:, q * 4:(q + 1) * 4, blk * 128:(blk + 1) * 128],
                        ps[b][:, :].rearrange("p (c n) -> p c n", c=4), [r_ps[b]], g.r_h[q * 4:(q + 1) * 4])
        P.barrier()

    def rmsnorm_fm(g, gi_vec, out3=None, r_out=None, out_f32=False):
        if out3 is None:
            out3, r_out = g.hnT, g.r_hn
        T = g.T
        sq = v3(yb_t[:, 0:KC * T], KC)
        rstd = scr_t[:, 4096:4096 + T]
        r_rstd = Res("rstd")
        for c in range(KC):
            eng = pool_ if c % 2 == 0 else dve_
            P.op(eng, lambda e, c=c: e.tensor_tensor(out=sq[:, c, :], in0=g.hT[:, c, :], in1=g.hT[:, c, :], op=ALU.mult),
                 reads=[g.r_h[c]], writes=[g.r_y[c]])
        for (c0, n) in g.tiles:
            b = next_ps()
            for c in range(KC):
                P.op(pe_, lambda e, b=b, c=c, c0=c0, n=n: e.matmul(ps[b][:, 0:n], lhsT=onesb, rhs=sq[:, c, c0:c0 + n],
                                                                    start=(c == 0), stop=(c == KC - 1)),
                     reads=[g.r_y[c], r_cst], writes=[r_ps[b]])
            P.op(act_, lambda e, b=b, c0=c0, n=n: e.activation(out=rstd[:, c0:c0 + n], in_=ps[b][:, 0:n], func=AF.Sqrt,
                                                               bias=EPS, scale=1.0 / D), reads=[r_ps[b]], writes=[r_rstd])
        P.op(dve_, lambda e: e.reciprocal(out=rstd, in_=rstd), reads=[r_rstd], writes=[r_rstd])
        for c in range(KC):
            eng = dve_
            P.op(eng, lambda e, c=c: e.scalar_tensor_tensor(out=out3[:, c, :], in0=g.hT[:, c, :], scalar=gvec[:, gi_vec, c:c + 1],
                                                            in1=rstd, op0=ALU.mult, op1=ALU.mult),
                 reads=[g.r_h[c], r_rstd, r_cst], writes=[r_out[c]])
        P.barrier()

    def resid_add_evac(g):
        def ev(m, c0, n, pst, rps):
            P.op(dve_, lambda e: e.tensor_tensor(out=g.hT[:, m, c0:c0 + n], in0=g.hT[:, m, c0:c0 + n], in1=pst[:, 0:n], op=ALU.add),
                 reads=[rps, g.r_h[m]], writes=[g.r_h[m]])
        return ev

    def gelu_ops(dst, src, tmp, reads, writes, r_tmp, e1=None, e2=None):
        e1 = e1 or pool_
        e2 = e2 or dve_
        P.op(e1, lambda e: e.tensor_tensor(out=tmp, in0=src, in1=src, op=ALU.mult), reads=reads, writes=[r_tmp])
        P.op(e1, lambda e: e.tensor_scalar(out=tmp, in0=tmp, scalar1=0.044715, scalar2=1.0, op0=ALU.mult, op1=ALU.add),
             reads=[r_tmp], writes=[r_tmp])
        P.op(e2, lambda e: e.tensor_tensor(out=tmp, in0=tmp, in1=src, op=ALU.mult), reads=reads + [r_tmp], writes=[r_tmp])
        P.op(act_, lambda e: e.activation(out=tmp, in_=tmp, func=AF.Sigmoid, scale=GC), reads=[r_tmp], writes=[r_tmp])
        P.op(e2, lambda e: e.tensor_tensor(out=dst, in0=tmp, in1=src, op=ALU.mult), reads=reads + [r_tmp], writes=writes)

    def attention(g):
        T = g.T
        qT = v3(yb_t[:, 0:KC * T], KC)
        r_q = g.r_y
        oT = g.hnT
        r_o = g.r_hn
        def evq(m, c0, n, pst, rps):
            copy_op(evac_eng(), qT[:, m, c0:c0 + n], pst[:, 0:n], [rps], [r_q[m]])
        linear_fm(lambda c0, nc_: w_qkv[:, c0:c0 + nc_], KC, 16, g.hnT, lambda k: [g.r_hn[k]], g.tiles, evq)
        ksb = scr_t[:, 8192:9216].bitcast(BF16)
        vsb = scr_t[:, 9216:10240].bitcast(BF16)
        r_ksb = Res("ksb"); r_vsb = Res("vsb")
        stgf = scr_t[:, 7680:8192]; r_stgf = Res("stgf2")
        if g.gi == 1:
            for ct in range(8):
                wv, rw = load_w(w_qkv[:, 2048 + ct * 512: 2048 + (ct + 1) * 512], KC, 512)
                b = next_ps()
                for k in range(KC):
                    P.op(pe_, lambda e, b=b, k=k, wv=wv: e.matmul(ps[b][:, :], lhsT=g.hnT[:, k, 512:640], rhs=wv[:, k, :],
                                                                 start=(k == 0), stop=(k == KC - 1)),
                         reads=[rw, g.r_hn[k]], writes=[r_ps[b]])
                dst = (k_s if ct < 4 else v_s)[:, (ct % 4) * 512:(ct % 4 + 1) * 512]
                copy_op(act_, stgf, ps[b][:, :], [r_ps[b]], [r_stgf, r_ps[b]])
                P.op(sp_, lambda e, dst=dst: e.dma_start(out=dst, in_=stgf), reads=[r_stgf], dma=True, sem_on=r_stgf)
                tgt, rt = (ksb, r_ksb) if ct < 4 else (vsb, r_vsb)
                copy_op(dve_, tgt[:, (ct % 4) * 512:(ct % 4 + 1) * 512], ps[b][:, :], [r_ps[b]], [rt, r_ps[b]])
        P.barrier()
        chk("ab"[g.gi] + "3")
        ew = []
        for s in range(2):
            base = s * 2560
            ew.append(dict(
                e=scr_t[:, base:base + 512], sp=scr_t[:, base + 512:base + 1024], t1=scr_t[:, base + 1024:base + 1536],
                u=scr_t[:, base + 1536:base + 2048],
                spm=scr_t[:, base + 2048:base + 2304].bitcast(BF16), w=scr_t[:, base + 2304:base + 2560].bitcast(BF16),
                r={k: Res(f"ew{s}{k}") for k in ("e", "sp", "t1", "u", "spm", "w")}))
        ucnt = {"i": 0}

        def unit(zfill, n, qp, kp, acc, r_acc, first, pvfn):
            s = ucnt["i"] % 2
            ucnt["i"] += 1
            t = ew[s]; r = t["r"]
            zb = s
            zfill(ps[zb], r_ps[zb])
            z = ps[zb][:, 0:n]
            P.op(act_, lambda e: e.activation(out=t["e"][:, 0:n], in_=z, func=AF.Exp, scale=SCALE), reads=[r_ps[zb]], writes=[r["e"], r_ps[zb]])
            P.op(act_, lambda e: e.activation(out=t["sp"][:, 0:n], in_=t["e"][:, 0:n], func=AF.Ln, bias=1.0, scale=1.0),
                 reads=[r["e"]], writes=[r["sp"]])
            P.op(dve_, lambda e: e.scalar_tensor_tensor(out=t["spm"][:, 0:n], in0=qp, scalar=kp, in1=t["sp"][:, 0:n],
                                                         op0=ALU.is_gt, op1=ALU.mult), reads=[r["sp"], r_cst], writes=[r["spm"]])
            P.op(pe_, lambda e: e.matmul(acc[:, 0:n], lhsT=trib, rhs=t["spm"][:, 0:n], start=first, stop=False, skip_group_check=True),
                 reads=[r["spm"], r_cst], writes=[r_acc])
            P.op(dve_, lambda e: e.scalar_tensor_tensor(out=t["t1"][:, 0:n], in0=z, scalar=SCALE, in1=t["sp"][:, 0:n],
                                                        op0=ALU.mult, op1=ALU.subtract), reads=[r_ps[zb], r["sp"]], writes=[r["t1"], r_ps[zb]])
            P.op(dve_, lambda e: e.tensor_tensor(out=t["u"][:, 0:n], in0=t["t1"][:, 0:n], in1=acc[:, 0:n], op=ALU.subtract),
                 reads=[r["t1"], r_acc], writes=[r["u"]])
            P.op(pe_, lambda e: e.matmul(acc[:, 0:n], lhsT=trib_c, rhs=t["spm"][:, 0:n], start=False, stop=False, skip_group_check=True),
                 reads=[r["spm"], r["u"], r_cst], writes=[r_acc])
            P.op(act_, lambda e: e.activation(out=t["e"][:, 0:n], in_=t["u"][:, 0:n], func=AF.Exp), reads=[r["u"]], writes=[r["e"]])
            P.op(dve_, lambda e: e.scalar_tensor_tensor(out=t["w"][:, 0:n], in0=qp, scalar=kp, in1=t["e"][:, 0:n],
                                                        op0=ALU.is_gt, op1=ALU.mult), reads=[r["e"], r_cst], writes=[r["w"]])
            pvfn(t["w"], r["w"])

        trib_c = tmpc[:, 0:64].bitcast(BF16)
        order = list(range(7, -1, -1)) + list(range(8, 32))
        qtiles = [(0, 384), (384, 384)] if g.gi == 0 else [(0, 512)]
        qoff = 0 if g.gi == 0 else 768
        kv_pend = None

        def load_kv(h):
            i = rr["wb"]; rr["wb"] ^= 1
            kt = wb_t[i][:, 0:4096]
            vh = wb_t[i][:, 4096:8192].rearrange("p (b d) -> p b d", b=NKB)
            P.op(sp_, lambda e: e.dma_start(out=kt, in_=KT[h]), reads=[r_KT], writes=[r_wb[i]], dma=True, sem_on=r_wb[i])
            P.op(sp_, lambda e: e.dma_start(out=vh, in_=VS[:, h * 128:(h + 1) * 128].rearrange("(b p) d -> p b d", p=128)),
                 reads=[r_VS], writes=[r_wb[i]], dma=True, sem_on=r_wb[i])
            return kt, vh, r_wb[i]

        sweep = {"i": 0}
        kv_pend = load_kv(0)
        for h in range(H):
            kt, vh, rkv = kv_pend
            if h + 1 < H:
                kv_pend = load_kv(h + 1)
            for (c0, n) in qtiles:
                si = sweep["i"] % 2
                sweep["i"] += 1
                acc, r_acc = ps[2 + si], r_ps[2 + si]
                ob, r_ob = ps[4 + si], r_ps[4 + si]
                for oi, i in enumerate(order):
                    def zfill(pz, rpz, i=i, c0=c0, n=n, h=h, kt=kt, rkv=rkv):
                        P.op(pe_, lambda e: e.matmul(pz[:, 0:n], lhsT=kt[:, i * 128:(i + 1) * 128], rhs=qT[:, h, c0:c0 + n],
                                                     start=True, stop=True), reads=[rkv, r_q[h]], writes=[rpz])

                    def pvfn(w, rw_, i=i, n=n, oi=oi, vh=vh, rkv=rkv, ob=ob, r_ob=r_ob):
                        P.op(pe_, lambda e: e.matmul(ob[:, 0:n], lhsT=vh[:, i, :], rhs=w[:, 0:n], start=(oi == 0), stop=(oi == NKB - 1),
                                                     skip_group_check=True), reads=[rkv, rw_], writes=[r_ob])
                    unit(zfill, n, qpos_t[:, qoff + c0: qoff + c0 + n], kpos[:, i:i + 1], acc, r_acc, oi == 0, pvfn)
                copy_op(evac_eng(), oT[:, h, c0:c0 + n], ob[:, 0:n], [r_ob], [r_o[h]])
            if h == 0:
                chk("ab"[g.gi] + "4")
        if g.gi == 1:
            ckb = [wb_t[0][:, 0:2048], wb_t[1][:, 0:2048]]
            cvb = [wb_t[0][:, 2048:4096], wb_t[1][:, 2048:4096]]
            ckT = [wb_t[0][:, 4096:6144].rearrange("p (h k) -> p h k", h=H), wb_t[1][:, 4096:6144].rearrange("p (h k) -> p h k", h=H)]
            r_ckT = [Res("ckT0"), Res("ckT1")]
            r_ck = [Res("ck0"), Res("ck1")]; r_cvv = [Res("cv0"), Res("cv1")]
            P.barrier()
            for bi in range(2):
                qc0 = 512 + bi * 64
                accs = [(ps[2], r_ps[2]), (ps[3], r_ps[3])]
                osum = [scr_t[:, 5120:5632], scr_t[:, 5632:6144]]
                r_osum = [Res("osum0"), Res("osum1")]
                obs = [(ps[4], r_ps[4]), (ps[5], r_ps[5])]
                seq = [32] + list(range(31, -1, -1))
                for oi, i in enumerate(seq):
                    sl = oi % 2
                    if i == 32 or stop_after == "b5x":
                        ksrc, vsrc, rks, rvs = ksb, vsb, r_ksb, r_vsb
                        kcol = kposs[:, 32 + bi:33 + bi]
                    else:
                        for qq in range(4):
                            P.op(pool_, lambda e, sl=sl, i=i, bi=bi, qq=qq: e.dma_start(out=ckb[sl][:, qq * 512:(qq + 1) * 512],
                                                                                     in_=ck_d[bi, i * 128:(i + 1) * 128, qq * 512:(qq + 1) * 512]),
                                 writes=[r_ck[sl]], dma=True, sem_on=r_ck[sl])
                            P.op(pool_, lambda e, sl=sl, i=i, bi=bi, qq=qq: e.dma_start(out=cvb[sl][:, qq * 512:(qq + 1) * 512],
                                                                                     in_=cv_d[bi, i * 128:(i + 1) * 128, qq * 512:(qq + 1) * 512]),
                                 writes=[r_cvv[sl]], dma=True, sem_on=r_cvv[sl])
                        ksrc, vsrc, rks, rvs = ckb[sl], cvb[sl], r_ck[sl], r_cvv[sl]
                        kcol = kposs[:, i:i + 1]
                    for hh in range(2):
                        b = 6 + hh
                        pb = ps[b][:].bitcast(BF16)
                        for c8 in range(8):
                            c = hh * 8 + c8
                            P.op(pe_, lambda e, pb=pb, c8=c8, c=c, ksrc=ksrc: e.transpose(out=pb[:, c8 * 128:(c8 + 1) * 128],
                                                                                         in_=ksrc[:, c * 128:(c + 1) * 128], identity=identb),
                                 reads=[rks, r_cst], writes=[r_ps[b]])
                        copy_op(evac_eng(), ckT[sl][:, hh * 8:(hh + 1) * 8, :], pb.rearrange("p (c n) -> p c n", c=8), [r_ps[b]], [r_ckT[sl]])
                    for half in range(2):
                        def zfill(pz, rpz, half=half, sl=sl, qc0=qc0):
                            for h8 in range(8):
                                h = half * 8 + h8
                                P.op(pe_, lambda e, h=h, h8=h8: e.matmul(pz[:, h8 * 64:(h8 + 1) * 64], lhsT=ckT[sl][:, h, :],
                                                                        rhs=qT[:, h, qc0:qc0 + 64], start=True, stop=True),
                                     reads=[r_ckT[sl], r_q[h]], writes=[rpz])

                        def pvfn(w, rw_, half=half, vsrc=vsrc, rvs=rvs, oi=oi):
                            ob, r_ob = obs[half]
                            for h8 in range(8):
                                h = half * 8 + h8
                                P.op(pe_, lambda e, h=h, h8=h8: e.matmul(ob[:, h8 * 64:(h8 + 1) * 64], lhsT=vsrc[:, h * 128:(h + 1) * 128],
                                                                        rhs=w[:, h8 * 64:(h8 + 1) * 64], start=True, stop=True),
                                     reads=[rvs, rw_], writes=[r_ob])
                            if oi == 0:
                                copy_op(dve_, osum[half], ob[:, :], [r_ob], [r_osum[half], r_ob])
                            else:
                                P.op(dve_, lambda e: e.tensor_tensor(out=osum[half], in0=osum[half], in1=ob[:, :], op=ALU.add),
                                     reads=[r_ob, r_osum[half]], writes=[r_osum[half], r_ob])
                        if stop_after != "b5y":
                            unit(zfill, 512, qpos_t[:, 1280:1792], kcol, accs[half][0], accs[half][1], oi == 0, pvfn)
                for half in range(2):
                    for h8 in range(8):
                        h = half * 8 + h8
                        copy_op(evac_eng(), oT[:, h, qc0:qc0 + 64], osum[half][:, h8 * 64:(h8 + 1) * 64], [r_osum[half]], [r_o[h]])
        P.barrier()
        linear_fm(lambda c0, nc_: w_osb[:, c0:c0 + nc_], KC, 16, oT, lambda k: [r_o[k]], g.tiles, resid_add_evac(g))
        P.barrier()

    def conv_ffn(g, l):
        T = g.T
        rmsnorm_fm(g, 2 + l)
        nseg = len(g.segs)
        EW = T + 2 * nseg
        aext = [scr_t[:, 0:EW], scr_t[:, 800:800 + EW]]
        acc_t = [scr_t[:, 1600:1600 + EW], scr_t[:, 2400:2400 + EW]]
        gl_t = [scr_t[:, 3200:3200 + T], scr_t[:, 4864 + 0:4864 + T]]
        tmp_t = [scr_t[:, 5700:5700 + EW], scr_t[:, 6500:6500 + EW]]
        r_ae = [Res("ae0"), Res("ae1")]; r_ac = [Res("ac0"), Res("ac1")]; r_gl = [Res("gl0"), Res("gl1")]; r_tm = [Res("tm0"), Res("tm1")]
        offs = []
        o = 0
        for (c0, n, kind) in g.segs:
            offs.append(o)
            o += n + 2
        jj = {"i": 0}
        for halfd in range(2):
            for sc in range(6 if halfd == 0 else 5):
                scg = halfd * 6 + sc
                wa, rwa = load_w(w_up[l][:, scg * 512:(scg + 1) * 512], KC, 512)
                wg, rwg = load_w(w_up[l][:, DFF + scg * 512: DFF + (scg + 1) * 512], KC, 512)
                for j4 in range(4):
                    j = scg * 4 + j4
                    jl = j - halfd * 24
                    s = jj["i"] % 2
                    jj["i"] += 1
                    for si, (c0, n, kind) in enumerate(g.segs):
                        dst = aext[s][:, offs[si]:offs[si] + 2]
                        if kind == "zero":
                            P.op(pool_, lambda e, dst=dst: e.memset(dst, 0.0), writes=[r_ae[s]])
                        elif kind == "save":
                            P.op(pool_, lambda e, dst=dst, j=j: e.tensor_copy(out=dst, in_=convsave[l][:, j, :]), reads=[r_cst], writes=[r_ae[s]])
                        else:
                            bidx = 0 if kind == "s0" else 1
                            P.op(pool_, lambda e, dst=dst, j=j, bidx=bidx: e.tensor_copy(out=dst, in_=sconv[l][bidx][:, j, :]), reads=[r_cst], writes=[r_ae[s]])
                    gps = []
                    for ti, (c0, n) in enumerate(g.tiles):
                        b = next_ps()
                        for k in range(KC):
                            P.op(pe_, lambda e, b=b, k=k, c0=c0, n=n, wa=wa, j4=j4: e.matmul(ps[b][:, 0:n], lhsT=wa[:, k, j4 * 128:(j4 + 1) * 128],
                                                                                           rhs=g.hnT[:, k, c0:c0 + n], start=(k == 0), stop=(k == KC - 1)),
                                 reads=[rwa, g.r_hn[k]], writes=[r_ps[b]])
                        for si, (s0, sn, kind) in enumerate(g.segs):
                            lo = max(c0, s0); hi = min(c0 + n, s0 + sn)
                            if lo < hi:
                                copy_op(act_, aext[s][:, offs[si] + 2 + lo - s0: offs[si] + 2 + hi - s0], ps[b][:, lo - c0:hi - c0],
                                        [r_ps[b]], [r_ae[s]])
                        b2 = next_ps()
                        for k in range(KC):
                            P.op(pe_, lambda e, b2=b2, k=k, c0=c0, n=n, wg=wg, j4=j4: e.matmul(ps[b2][:, 0:n], lhsT=wg[:, k, j4 * 128:(j4 + 1) * 128],
                                                                                             rhs=g.hnT[:, k, c0:c0 + n], start=(k == 0), stop=(k == KC - 1)),
                                 reads=[rwg, g.r_hn[k]], writes=[r_ps[b2]])
                        copy_op(dve_ if ti == 0 else act_, gl_t[s][:, c0:c0 + n], ps[b2][:, 0:n], [r_ps[b2]], [r_gl[s]])
                    if g.gi == 0:
                        P.op(pool_, lambda e, j=j, s=s: e.tensor_copy(out=convsave[l][:, j, :], in_=aext[s][:, 2 + 766:2 + 768]),
                             reads=[r_ae[s]], writes=[r_cst])
                    else:
                        for si in range(3):
                            n_ = g.segs[si][1]
                            P.op(pool_, lambda e, j=j, s=s, si=si, n_=n_: e.tensor_copy(out=aout[l][:, j, 2 * si:2 * si + 2],
                                                                                      in_=aext[s][:, offs[si] + n_:offs[si] + n_ + 2]),
                                 reads=[r_ae[s]], writes=[r_cst])
                    W_ = EW - 2
                    P.op(act_, lambda e, s=s, j=j: e.activation(out=acc_t[s][:, 0:W_], in_=aext[s][:, 2:EW], func=AF.Identity,
                                                               bias=convb[l][:, j:j + 1], scale=convw[l][:, j, 2:3]),
                         reads=[r_ae[s], r_cst], writes=[r_ac[s]])
                    P.op(dve_, lambda e, s=s, j=j: e.scalar_tensor_tensor(out=acc_t[s][:, 0:W_], in0=aext[s][:, 1:EW - 1], scalar=convw[l][:, j, 1:2],
                                                                         in1=acc_t[s][:, 0:W_], op0=ALU.mult, op1=ALU.add),
                         reads=[r_ae[s], r_ac[s], r_cst], writes=[r_ac[s]])
                    P.op(dve_, lambda e, s=s, j=j: e.scalar_tensor_tensor(out=acc_t[s][:, 0:W_], in0=aext[s][:, 0:EW - 2], scalar=convw[l][:, j, 0:1],
                                                                          in1=acc_t[s][:, 0:W_], op0=ALU.mult, op1=ALU.add),
                         reads=[r_ae[s], r_ac[s], r_cst], writes=[r_ac[s]])
                    jy = j - halfd * 24
                    P.op(pool_, lambda e, s=s: e.tensor_tensor(out=tmp_t[s][:, 0:W_], in0=acc_t[s][:, 0:W_], in1=acc_t[s][:, 0:W_], op=ALU.mult),
                         reads=[r_ac[s]], writes=[r_tm[s]])
                    P.op(pool_, lambda e, s=s: e.tensor_scalar(out=tmp_t[s][:, 0:W_], in0=tmp_t[s][:, 0:W_], scalar1=0.044715, scalar2=1.0,
                                                              op0=ALU.mult, op1=ALU.add), reads=[r_tm[s]], writes=[r_tm[s]])
                    P.op(dve_, lambda e, s=s: e.tensor_tensor(out=tmp_t[s][:, 0:W_], in0=tmp_t[s][:, 0:W_], in1=acc_t[s][:, 0:W_], op=ALU.mult),
                         reads=[r_tm[s], r_ac[s]], writes=[r_tm[s]])
                    P.op(act_, lambda e, s=s: e.activation(out=tmp_t[s][:, 0:W_], in_=tmp_t[s][:, 0:W_], func=AF.Sigmoid, scale=GC),
                         reads=[r_tm[s]], writes=[r_tm[s]])
                    P.op(dve_, lambda e, s=s: e.tensor_tensor(out=tmp_t[s][:, 0:W_], in0=tmp_t[s][:, 0:W_], in1=acc_t[s][:, 0:W_], op=ALU.mult),
                         reads=[r_tm[s], r_ac[s]], writes=[r_tm[s]])
                    for si, (s0, sn, kind) in enumerate(g.segs):
                        P.op(pool_ if si == 0 else dve_, lambda e, s=s, si=si, s0=s0, sn=sn, jy=jy: e.tensor_tensor(
                            out=g.yb[:, jy, s0:s0 + sn], in0=tmp_t[s][:, offs[si]:offs[si] + sn], in1=gl_t[s][:, s0:s0 + sn], op=ALU.mult),
                            reads=[r_tm[s], r_gl[s]], writes=[g.r_y[jy]])
            P.barrier()
            nk = 24 if halfd == 0 else 20
            k0 = halfd * 24
            wc = 256
            linear_fm(lambda c0, nc_, k0=k0, nk=nk: w_down[l][k0 * 128:(k0 + nk) * 128, c0:c0 + nc_], nk, 16, g.yb,
                      lambda k: [g.r_y[k]], g.tiles, resid_add_evac(g), wcols=wc)
            P.barrier()

    def ple(g, l):
        T = g.T
        rmsnorm_fm(g, 4 + l)
        pT = v3(yb_t[:, 16 * T:18 * T], 2)
        r_pT = Res("pT")
        pst = [scr_t[:, 0:256], scr_t[:, 256:512]]
        r_pst = [Res("pst0"), Res("pst1")]
        for blk in range(g.nblk):
            if g.gi == 0:
                src = pm[l, blk * 128:(blk + 1) * 128, :]
            elif blk < 4:
                src = pm[l, 768 + blk * 128:768 + (blk + 1) * 128, :]
            else:
                src = psm[l]
            xi = blk % 2
            P.op(sp_, lambda e, xi=xi, src=src: e.dma_start(out=pst[xi], in_=src), writes=[r_pst[xi]], dma=True, sem_on=r_pst[xi])
            b = next_ps()
            for c in range(2):
                P.op(pe_, lambda e, b=b, c=c, xi=xi: e.transpose(out=ps[b][:, c * 128:(c + 1) * 128], in_=pst[xi][:, c * 128:(c + 1) * 128],
                                                                identity=identf), reads=[r_pst[xi], r_cst], writes=[r_ps[b]])
            copy_op(evac_eng(), pT[:, :, blk * 128:(blk + 1) * 128], ps[b][:, 0:256].rearrange("p (c n) -> p c n", c=2), [r_ps[b]], [r_pT])
        wp, rwp = None, None
        sig = [scr_t[:, 1024:1536], scr_t[:, 1536:2048]]; r_sig = [Res("sig0"), Res("sig1")]
        pet = [scr_t[:, 2048:2560], scr_t[:, 2560:3072]]; r_pet = [Res("pet0"), Res("pet1")]
        wpl = scr_t[:, 6144:8192].bitcast(BF16).rearrange("p (k n) -> p k n", k=2)
        r_wpl = Res("wpl")
        P.op(pool_, lambda e: e.dma_start(out=wpl, in_=w_ple[l].rearrange("(k p) n -> p k n", p=128)), writes=[r_wpl], dma=True, sem_on=r_wpl)
        cnt = {"i": 0}

        def ev(m, c0, n, pst_, rps):
            s = cnt["i"] % 2
            cnt["i"] += 1
            P.op(act_, lambda e: e.activation(out=sig[s][:, 0:n], in_=pst_[:, 0:n], func=AF.Sigmoid), reads=[rps], writes=[r_sig[s]])
            b = next_ps()
            for k in range(2):
                P.op(pe_, lambda e, k=k: e.matmul(ps[b][:, 0:n], lhsT=wpl[:, k, m * 128:(m + 1) * 128], rhs=pT[:, k, c0:c0 + n],
                                                  start=(k == 0), stop=(k == 1)), reads=[r_wpl, r_pT], writes=[r_ps[b]])
            P.op(dve_, lambda e: e.tensor_tensor(out=pet[s][:, 0:n], in0=sig[s][:, 0:n], in1=ps[b][:, 0:n], op=ALU.mult),
                 reads=[r_sig[s], r_ps[b]], writes=[r_pet[s]])
            P.op(pool_, lambda e: e.tensor_tensor(out=g.hT[:, m, c0:c0 + n], in0=g.hT[:, m, c0:c0 + n], in1=pet[s][:, 0:n], op=ALU.add),
                 reads=[r_pet[s], g.r_h[m]], writes=[g.r_h[m]])
        linear_fm(lambda c0, nc_: w_gate[l][:, c0:c0 + nc_], KC, 16, g.hnT, lambda k: [g.r_hn[k]], g.tiles, ev)
        P.barrier()

    def chunk_mlp(g):
        T = g.T
        rmsnorm_fm(g, 1)
        uT = v3(yb_t[:, 0:KC * T], KC); r_u = g.r_y
        gtmp = [scr_t[:, 0:512], scr_t[:, 512:1024]]; r_gt = [Res("gt0"), Res("gt1")]
        gsrc = [scr_t[:, 1024:1536], scr_t[:, 1536:2048]]; r_gs = [Res("gs0"), Res("gs1")]
        cnt = {"i": 0}

        def evu(m, c0, n, pst_, rps):
            s = cnt["i"] % 2
            cnt["i"] += 1
            copy_op(act_, gsrc[s][:, 0:n], pst_[:, 0:n], [rps], [r_gs[s]])
            gelu_ops(uT[:, m, c0:c0 + n], gsrc[s][:, 0:n], gtmp[s][:, 0:n], [r_gs[s]], [r_u[m]], r_gt[s])
        linear_fm(lambda c0, nc_: w_in[:, c0:c0 + nc_], KC, 16, g.hnT, lambda k: [g.r_hn[k]], g.tiles, evu)
        nb = g.nblk
        vtok = scr_t[:, 2048:2048 + nb * 1024].bitcast(BF16).rearrange("p (b n) -> p b n", b=nb)
        r_vt = [Res(f"vt{b}") for b in range(nb)]
        ssq = tmpc[:, 64:64 + 4 * nb].rearrange("p (b c) -> p b c", b=nb)
        r_ssq = Res("ssq")
        vf = scr_t[:, 8192:10240]; r_vf = Res("vf")
        P.op(pool_, lambda e: e.memset(tmpc[:, 64:124], 0.0), writes=[r_ssq])
        gvb = None
        for ct in range(4):
            wv, rw = load_w(w_in[:, 2048 + ct * 512:2048 + (ct + 1) * 512], KC, 512)
            for blk in range(nb):
                b = next_ps()
                for k in range(KC):
                    P.op(pe_, lambda e, b=b, k=k, blk=blk, wv=wv: e.matmul(ps[b][:, :], lhsT=g.hnT[:, k, blk * 128:(blk + 1) * 128], rhs=wv[:, k, :],
                                                                          start=(k == 0), stop=(k == KC - 1)), reads=[rw, g.r_hn[k]], writes=[r_ps[b]])
                s = cnt["i"] % 2
                cnt["i"] += 1
                copy_op(act_, gsrc[s], ps[b][:, :], [r_ps[b]], [r_gs[s]])
                is_s = (g.gi == 1 and blk == 4)
                if is_s:
                    gdst, gw = vf[:, ct * 512:(ct + 1) * 512], [r_vf]
                else:
                    gdst, gw = gsrc[s], [r_gs[s]]
                gelu_ops(gdst, gsrc[s], gtmp[s], [r_gs[s]], gw, r_gt[s])
                P.op(act_, lambda e, gdst=gdst, s=s, blk=blk, ct=ct: e.activation(out=gtmp[s], in_=gdst, func=AF.Square, accum_out=ssq[:, blk, ct:ct + 1]),
                     reads=gw + [r_ssq], writes=[r_gt[s], r_ssq])
                copy_op(dve_, vtok[:, blk, ct * 512:(ct + 1) * 512], gdst, gw, [r_vt[blk]])
        P.barrier()
        st = tmpc[:, 100:100 + 2 * nb].rearrange("p (b c) -> p b c", b=nb)
        P.op(dve_, lambda e: e.tensor_reduce(out=st[:, :, 0], in_=ssq, axis=mybir.AxisListType.X, op=ALU.add), reads=[r_ssq], writes=[r_ssq])
        P.op(act_, lambda e: e.activation(out=st[:, :, 1], in_=st[:, :, 0], func=AF.Sqrt, bias=EPS, scale=1.0 / D), reads=[r_ssq], writes=[r_ssq])
        P.op(dve_, lambda e: e.reciprocal(out=st[:, :, 1], in_=st[:, :, 1]), reads=[r_ssq], writes=[r_ssq])
        gvb = scr_t[:, 0:2048]; r_gvb = Res("gvb")
        P.op(sp_, lambda e: e.dma_start(out=gvb, in_=gvb_d), writes=[r_gvb], dma=True, sem_on=r_gvb)
        vn = hnT_t[:, 0:2048]
        vns = [hnT_t[:, 0:2048], hnT_t[:, 2048:4096]]
        r_vn = [Res("vn0"), Res("vn1")]
        wsp = wsp_t[:].rearrange("p (v g t) -> p v g t", v=2, g=8)
        bsh = bs_hl[:].rearrange("p (v g t) -> p v g t", v=2, g=8)
        ymlp = v3(hnT_t[:, 4096:4096 + 0], 1) if False else None
        for blk in range(nb):
            s = blk % 2
            is_s = (g.gi == 1 and blk == 4)
            var = 1 if is_s else 0
            if is_s:
                P.op(dve_, lambda e, blk=blk: e.scalar_tensor_tensor(out=vf, in0=vf, scalar=st[:, blk, 1:2], in1=gvb, op0=ALU.mult, op1=ALU.mult),
                     reads=[r_vf, r_ssq, r_gvb], writes=[r_vf])
                P.op(sp_, lambda e: e.dma_start(out=mlpv_s, in_=vf), reads=[r_vf], dma=True, sem_on=r_vf)
                copy_op(act_, vns[s], vf, [r_vf], [r_vn[s]])
            else:
                P.op(dve_, lambda e, blk=blk, s=s: e.scalar_tensor_tensor(out=vns[s], in0=vtok[:, blk, :], scalar=st[:, blk, 1:2], in1=gvb,
                                                                         op0=ALU.mult, op1=ALU.mult), reads=[r_vt[blk], r_ssq, r_gvb], writes=[r_vn[s]])
            for m4 in range(4):
                b = next_ps()
                for mi in range(4):
                    m = m4 * 4 + mi
                    gg = m // 2
                    P.op(pe_, lambda e, b=b, mi=mi, m=m, gg=gg, s=s, var=var: e.matmul(ps[b][:, mi * 128:(mi + 1) * 128], lhsT=vns[s][:, m * 128:(m + 1) * 128],
                                                                                     rhs=wsp[:, var, gg, :], start=True, stop=False),
                         reads=[r_vn[s], r_bs], writes=[r_ps[b]])
                    P.op(pe_, lambda e, b=b, mi=mi, gg=gg, var=var: e.matmul(ps[b][:, mi * 128:(mi + 1) * 128], lhsT=ones2[:, :], rhs=bsh[:, var, gg, :],
                                                                           start=False, stop=True), reads=[r_bs, r_cst], writes=[r_ps[b]])
                P.op(dve_, lambda e, b=b, m4=m4, blk=blk: e.tensor_tensor(out=uT[:, m4 * 4:(m4 + 1) * 4, blk * 128:(blk + 1) * 128],
                                                                         in0=uT[:, m4 * 4:(m4 + 1) * 4, blk * 128:(blk + 1) * 128],
                                                                         in1=ps[b][:, :].rearrange("p (c n) -> p c n", c=4), op=ALU.mult),
                     reads=[r_ps[b]] + r_u[m4 * 4:(m4 + 1) * 4], writes=r_u[m4 * 4:(m4 + 1) * 4])
        P.barrier()
        linear_fm(lambda c0, nc_: w_omlp[:, c0:c0 + nc_], KC, 16, uT, lambda k: [r_u[k]], g.tiles, resid_add_evac(g))
        P.barrier()

    def final_out(g):
        rmsnorm_fm(g, 6, out3=g.hT, r_out=g.r_h)
        ost = [scr_t[:, 0:2048], scr_t[:, 2048:4096]]
        r_ost = [Res("ost0"), Res("ost1")]
        for blk in range(g.nblk):
            if g.gi == 0:
                if blk < 2:
                    continue
                dst = y_m[(blk - 2) * 128:(blk - 1) * 128, :]
            elif blk < 4:
                dst = y_m[512 + blk * 128:512 + (blk + 1) * 128, :]
            else:
                dst = y_s
            s = blk % 2
            for q in range(4):
                b = next_ps()
                for c4 in range(4):
                    c = q * 4 + c4
                    P.op(pe_, lambda e, b=b, c4=c4, c=c, blk=blk: e.transpose(out=ps[b][:, c4 * 128:(c4 + 1) * 128],
                                                                             in_=g.hT[:, c, blk * 128:(blk + 1) * 128], identity=identf),
                         reads=[g.r_h[c], r_cst], writes=[r_ps[b]])
                copy_op(evac_eng(), ost[s][:, q * 512:(q + 1) * 512], ps[b][:, :], [r_ps[b]], [r_ost[s]])
            P.op(sp_, lambda e, dst=dst, s=s: e.dma_start(out=dst, in_=ost[s]), reads=[r_ost[s]], dma=True, sem_on=r_ost[s])
        if g.gi == 1:
            co = scr_t[0:6, 4096:4096 + DFF]
            r_co = Res("co")
            for l in range(2):
                for j in range(FC):
                    b = next_ps()
                    P.op(pe_, lambda e, b=b, j=j, l=l: e.transpose(out=ps[b][0:6, 0:128], in_=aout[l][:, j, :], identity=identf),
                         reads=[r_cst], writes=[r_ps[b]])
                    copy_op(evac_eng(), co[:, j * 128:(j + 1) * 128], ps[b][0:6, 0:128], [r_ps[b]], [r_co])
                P.op(sp_, lambda e, l=l: e.dma_start(out=conv_o[l], in_=co), reads=[r_co], dma=True, sem_on=r_co)
        P.barrier()

    setup_consts()
    trib_c0 = tmpc[:, 0:64].bitcast(BF16)
    P.op(dve_, lambda e: e.tensor_scalar(out=trib_c0, in0=trib, scalar1=-1.0, scalar2=1.0, op0=ALU.mult, op1=ALU.add),
         reads=[r_cst], writes=[r_cst])
    P.barrier()
    if stop_after != "consts":
        phase_kv()
    if not (stop_after or "").startswith(("kv", "consts")):
        try:
            for gi in range(2):
                g = make_group(gi)
                pre = "ab"[gi]
                load_x(g)
                chk(pre + "1")
                rmsnorm_fm(g, 0)
                chk(pre + "2")
                attention(g)
                chk(pre + "5")
                chk(pre + "5x")
                chk(pre + "5y")
                conv_ffn(g, 0)
                chk(pre + "6")
                ple(g, 0)
                chk(pre + "7")
                chunk_mlp(g)
                chk(pre + "8")
                conv_ffn(g, 1)
                ple(g, 1)
                final_out(g)
                chk(pre + "9")
        except _Stop:
            P.barrier()
    P.emit()
    return nc, P


def _chunk_layout(v):
    return np.ascontiguousarray(v.reshape(-1, 128).T)


_CACHE = {}


def kernel(x_prompt, x_sample, p_prompt, p_sample, cache_k, cache_v, state_conv, g_mix, w_qkv, w_o_sb,
           w_in_mlp, g_v_mlp, w_s_mlp, b_s_mlp, w_o_mlp, g_ffn, w_up, conv_w, conv_b, w_down, g_ple, w_ple,
           w_ple_gate, g_final, _stop_after=None):
    f = lambda a: np.ascontiguousarray(np.asarray(a, dtype=np.float32))
    x_prompt, x_sample, p_prompt, p_sample = f(x_prompt), f(x_sample), f(p_prompt), f(p_sample)
    cache_k, cache_v, state_conv = f(cache_k), f(cache_v), f(state_conv)
    key = _stop_after
    if key not in _CACHE:
        _CACHE[key] = build_nc(_stop_after)
    nc, P = _CACHE[key]

    gvec = np.zeros((128, 8, 16), np.float32)
    for i, v in enumerate([g_mix[0], g_mix[1], g_ffn[0], g_ffn[1], g_ple[0], g_ple[1], g_final]):
        gvec[:, i, :] = _chunk_layout(f(v))
    shared = {
        "w_qkv": f(w_qkv[0]), "w_osb": f(w_o_sb[0]), "w_in": f(w_in_mlp[0]), "w_omlp": f(w_o_mlp[0]),
        "w_up": f(w_up), "w_down": f(w_down), "w_ple": f(w_ple), "w_gate": f(w_ple_gate),
        "w_sT": np.ascontiguousarray(f(w_s_mlp[0]).transpose(0, 2, 1)),
        "gvec": gvec,
        "gmix0b": np.ascontiguousarray(np.broadcast_to(f(g_mix[0])[None, :], (128, D))),
        "gvb": np.ascontiguousarray(np.broadcast_to(f(g_v_mlp[0])[None, :], (128, D))),
        "convw": np.ascontiguousarray(f(conv_w).reshape(2, 3, FC, 128).transpose(0, 3, 2, 1)),
        "convb": np.ascontiguousarray(f(conv_b).reshape(2, FC, 128).transpose(0, 2, 1)),
        "bs": f(b_s_mlp).reshape(1, 8, 128),
    }
    pidx = np.arange(128, dtype=np.float32)
    in_maps = []
    for c in range(8):
        b, qd = c // 4, c % 4
        s = 1024 * qd
        own = list(range(8 * qd, 8 * qd + 8))
        prev = list(range(8 * qd - 1, -1, -1))
        fut = list(range(8 * qd + 8, 32))
        order = own + prev + fut
        xk = np.concatenate([x_prompt[b, i * 128:(i + 1) * 128] for i in order], 0)
        kpos = np.zeros((128, 34), np.float32)
        for li, i in enumerate(order):
            kpos[:, li] = i * 128 + pidx
        xm = np.zeros((1280, D), np.float32)
        pmm = np.zeros((2, 1280, PL), np.float32)
        lo = s - 256
        if lo >= 0:
            xm[:] = x_prompt[b, lo:s + 1024]
            pmm[:] = p_prompt[:, b, lo:s + 1024]
        else:
            xm[256:] = x_prompt[b, 0:1024]
            pmm[:, 256:] = p_prompt[:, b, 0:1024]
        qpos = np.zeros((128, 1280 + 512), np.float32)
        qp = (lo + np.arange(1280)).astype(np.float32)
        qp[qp < 0] = -1.0
        qpos[:, 0:1280] = qp[None, :]
        qpos[:, 1280:] = (4096 + (np.arange(512) % 64)).astype(np.float32)[None, :]
        kposs = np.zeros((128, 34), np.float32)
        for i in range(32):
            kposs[:, i] = i * 128 + pidx
        kposs[:, 32] = np.where(pidx < 64, 4096 + pidx, BIG)
        kposs[:, 33] = np.where(pidx >= 64, 4096 + pidx - 64, BIG)
        sb = [2 * c, 2 * c + 1]
        sconv = np.ascontiguousarray(state_conv[:, sb].reshape(2, 2, 2, FC, 128).transpose(0, 1, 4, 3, 2))
        m = dict(shared)
        m.update({
            "xk": xk, "kpos": kpos, "xm": xm, "xs": np.ascontiguousarray(x_sample[sb].reshape(128, D)),
            "pm": pmm, "psm": np.ascontiguousarray(p_sample[:, sb].reshape(2, 128, PL)),
            "qpos": qpos, "kposs": kposs,
            "ck": np.ascontiguousarray(cache_k[0, sb].reshape(2, 4096, D)),
            "cv": np.ascontiguousarray(cache_v[0, sb].reshape(2, 4096, D)),
            "sconv": sconv,
        })
        in_maps.append(m)
    res = run_bass_kernel_spmd(nc, in_maps, core_ids=list(range(8)))
    R = res.results
    y_prompt = np.zeros((2, 4096, D), np.float32)
    k_prompt = np.zeros((1, 2, 4096, H, 128), np.float32)
    v_prompt = np.zeros((1, 2, 4096, H, 128), np.float32)
    y_sample = np.zeros((16, 64, D), np.float32)
    k_sample = np.zeros((1, 16, 64, H, 128), np.float32)
    v_sample = np.zeros((1, 16, 64, H, 128), np.float32)
    mlpv = np.zeros((1, 16, 64, D), np.float32)
    conv_prompt = np.zeros((2, 2, 2, DFF), np.float32)
    conv_sample = np.zeros((2, 16, 2, DFF), np.float32)
    for c in range(8):
        b, qd = c // 4, c % 4
        s = 1024 * qd
        r = R[c]
        y_prompt[b, s:s + 1024] = r["y_m"]
        k_prompt[0, b, s:s + 1024] = r["k_m"].reshape(1024, H, 128)
        v_prompt[0, b, s:s + 1024] = r["v_m"].reshape(1024, H, 128)
        y_sample[2 * c:2 * c + 2] = r["y_s"].reshape(2, 64, D)
        k_sample[0, 2 * c:2 * c + 2] = r["k_s"].reshape(2, 64, H, 128)
        v_sample[0, 2 * c:2 * c + 2] = r["v_s"].reshape(2, 64, H, 128)
        mlpv[0, 2 * c:2 * c + 2] = r["mlpv_s"].reshape(2, 64, D)
        co = r["conv_o"]
        if qd == 3:
            conv_prompt[:, b] = co[:, 0:2]
        conv_sample[:, 2 * c] = co[:, 2:4]
        conv_sample[:, 2 * c + 1] = co[:, 4:6]
    return (y_prompt, y_sample, k_prompt, v_prompt, k_sample, v_sample, mlpv, conv_prompt, conv_sample)
```
